# Optimizing a Trainium2 kernel written in Bass

```python
import jax, jax.numpy as jnp
from jax import lax
import numpy as np

D_MODEL = 2048
BATCH = 2
SEQ = 16384
DEPTH = 2

N_META = 16
CHUNK = 64
MLP_HIDDEN = 4 * D_MODEL
NORM_EPS = 1e-6
N_BRANCH = 3
L2_EPS = 1e-24

H_WIDTH = D_MODEL // 2
H_HEAD_DIM = 128
H_HEADS = H_WIDTH // H_HEAD_DIM

M_WIDTH = D_MODEL // 2
M_HEAD_DIM = 64
M_HEADS = M_WIDTH // M_HEAD_DIM
M_GROUPS = 2
M_HEADS_PER_GROUP = M_HEADS // M_GROUPS
M_STATE = 128
M_CONV = 4
M_CONV_CH = M_WIDTH + 2 * M_GROUPS * M_STATE
M_DT_MIN = 1e-3
M_DT_MAX = 1e-1

R_WIDTH = D_MODEL // 2
R_HEAD_DIM = 64
R_HEADS = R_WIDTH // R_HEAD_DIM
R_DECAY_RANK = max(32, int(round(1.8 * D_MODEL ** 0.5 / 32)) * 32)
R_AAA_RANK = max(32, int(round(1.8 * D_MODEL ** 0.5 / 32)) * 32)
R_MV_RANK = max(32, int(round(1.3 * D_MODEL ** 0.5 / 32)) * 32)
R_GATE_RANK = max(32, int(round(0.6 * D_MODEL ** 0.8 / 32)) * 32)
R_GN_EPS = 64e-5

H_COLS = 4 * H_WIDTH
M_COLS = M_WIDTH + M_CONV_CH + M_HEADS
R_COLS = 3 * R_WIDTH + R_DECAY_RANK + R_AAA_RANK + R_GATE_RANK
GATE_COLS = N_BRANCH * D_MODEL
IN_COLS = H_COLS + M_COLS + R_COLS + GATE_COLS

kernel_name = 'hybrid_hgrn2_mamba2_rwkv7_gated_parallel'


def rmsnorm(x, w):
    xf = x.astype(jnp.float32)
    y = xf * lax.rsqrt(jnp.mean(xf * xf, -1, keepdims=True) + NORM_EPS)
    return (y * w.astype(jnp.float32)).astype(x.dtype)


def token_shift(p, mu):
    prev = jnp.pad(p, ((0, 0), (1, 0), (0, 0)))[:, :-1]
    return p + (prev - p) * mu


def causal_depthwise_conv(u, w, b):
    out = lax.conv_general_dilated(u, w[:, None, :], window_strides=(1,), padding=[(M_CONV - 1, 0)],
                                   dimension_numbers=('NWC', 'WIO', 'NWC'), feature_group_count=u.shape[-1])
    return out + b


def masked_exp(mask, logits):
    return jnp.where(mask, jnp.exp(jnp.where(mask, logits, 0.0)), 0.0)


def hgrn2_mixer(q_raw, f_raw, i_raw, g_raw, lb, norm_w, valid):
    b_, l_, _ = q_raw.shape
    nc = l_ // CHUNK
    f32 = jnp.float32
    q = jax.nn.silu(q_raw.astype(f32))
    fr = f_raw.astype(f32)
    logf = jnp.log(lb + (1.0 - lb) * jax.nn.sigmoid(fr))
    k = (1.0 - lb) * jax.nn.sigmoid(-fr) * valid
    v = i_raw.astype(f32)

    def chunks(t):
        return t.reshape(b_, nc, CHUNK, H_HEADS, H_HEAD_DIM).transpose(1, 0, 3, 2, 4)

    causal = jnp.tril(jnp.ones((CHUNK, CHUNK), bool))[:, :, None]

    def step(s, inp):
        qc, kc, vc, gc = inp
        gcum = jnp.cumsum(gc, axis=2)
        rel = masked_exp(causal, gcum[:, :, :, None, :] - gcum[:, :, None, :, :])
        attn = jnp.einsum('bhik,bhjk,bhijk->bhij', qc, kc, rel)
        o = jnp.einsum('bhij,bhjv->bhiv', attn, vc) + jnp.einsum('bhik,bhkv->bhiv', qc * jnp.exp(gcum), s)
        glast = gcum[:, :, -1:, :]
        s = jnp.exp(glast[:, :, 0, :, None]) * s + jnp.einsum('bhjk,bhjv->bhkv', kc * jnp.exp(glast - gcum), vc)
        return s, o

    s0 = jnp.zeros((b_, H_HEADS, H_HEAD_DIM, H_HEAD_DIM), f32)
    _, o = lax.scan(step, s0, (chunks(q), chunks(k), chunks(v), chunks(logf)))
    o = o.transpose(1, 0, 3, 2, 4).reshape(b_, l_, H_HEADS, H_HEAD_DIM)
    o = o * jax.nn.sigmoid(g_raw.astype(f32)).reshape(b_, l_, H_HEADS, H_HEAD_DIM)
    o = o * lax.rsqrt(jnp.mean(o * o, -1, keepdims=True) + NORM_EPS) * norm_w
    return o.reshape(b_, l_, H_WIDTH).astype(q_raw.dtype)


def ssd_chunked(x, dt, a, bmat, cmat):
    b_, l_, g_, hg_, p_ = x.shape
    nc = l_ // CHUNK
    xc = (x * dt[..., None]).reshape(b_, nc, CHUNK, g_, hg_, p_)
    acum = jnp.cumsum((dt * a).reshape(b_, nc, CHUNK, g_, hg_), axis=2)
    bc = bmat.reshape(b_, nc, CHUNK, g_, -1)
    cc = cmat.reshape(b_, nc, CHUNK, g_, -1)
    causal = jnp.tril(jnp.ones((CHUNK, CHUNK), bool))[:, :, None, None]
    seg = acum[:, :, :, None] - acum[:, :, None, :]
    lmat = masked_exp(causal, seg)
    scores = jnp.einsum('bzign,bzjgn->bzijg', cc, bc)
    y_diag = jnp.einsum('bzijg,bzijgh,bzjghp->bzighp', scores, lmat, xc)
    to_end = jnp.exp(acum[:, :, -1:] - acum)
    states = jnp.einsum('bzjgn,bzjgh,bzjghp->bzghpn', bc, to_end, xc)
    chunk_decay = jnp.exp(acum[:, :, -1])

    def carry_step(h, inp):
        st, dec = inp
        return h * dec[..., None, None] + st, h

    h0 = jnp.zeros((b_, g_, hg_, p_, bc.shape[-1]), xc.dtype)
    _, h_in = lax.scan(carry_step, h0, (jnp.moveaxis(states, 1, 0), jnp.moveaxis(chunk_decay, 1, 0)))
    h_in = jnp.moveaxis(h_in, 0, 1)
    y_off = jnp.einsum('bzign,bzghpn,bzigh->bzighp', cc, h_in, jnp.exp(acum))
    return (y_diag + y_off).reshape(b_, l_, g_, hg_, p_)


def mamba2_mixer(z, xbc, dt_raw, conv_w, conv_b, dt_bias, a_log, d_skip, norm_w, valid):
    b_, l_, _ = z.shape
    f32 = jnp.float32
    xbc = jax.nn.silu(causal_depthwise_conv(xbc, conv_w, conv_b).astype(f32))
    xs = xbc[..., :M_WIDTH].reshape(b_, l_, M_GROUPS, M_HEADS_PER_GROUP, M_HEAD_DIM)
    bmat = xbc[..., M_WIDTH:M_WIDTH + M_GROUPS * M_STATE].reshape(b_, l_, M_GROUPS, M_STATE)
    cmat = xbc[..., M_WIDTH + M_GROUPS * M_STATE:].reshape(b_, l_, M_GROUPS, M_STATE)
    dt = (jax.nn.softplus(dt_raw.astype(f32) + dt_bias) * valid).reshape(b_, l_, M_GROUPS, M_HEADS_PER_GROUP)
    a = -jnp.exp(a_log.astype(f32)).reshape(M_GROUPS, M_HEADS_PER_GROUP)
    y = ssd_chunked(xs, dt, a, bmat, cmat) + d_skip.reshape(M_GROUPS, M_HEADS_PER_GROUP)[..., None] * xs
    y = y.reshape(b_, l_, M_GROUPS, M_WIDTH // M_GROUPS) * jax.nn.silu(z.astype(f32)).reshape(b_, l_, M_GROUPS, M_WIDTH // M_GROUPS)
    y = y * lax.rsqrt(jnp.mean(y * y, -1, keepdims=True) + NORM_EPS)
    return (y.reshape(b_, l_, M_WIDTH) * norm_w).astype(z.dtype)


def rwkv7_mixer(r, decay, k, v, a, g, k_k, k_a, r_k, gn_w, gn_b, valid):
    b_, l_, _ = r.shape

    def heads(t):
        return t.reshape(b_, l_, R_HEADS, R_HEAD_DIM)

    kk = heads(k * k_k)
    kk = kk * lax.rsqrt(jnp.maximum(jnp.sum(kk * kk, -1, keepdims=True), L2_EPS))
    kh = heads(k * (1.0 + (a - 1.0) * k_a) * valid)
    rh, vh, ah = heads(r), heads(v), heads(a)

    def seq(t):
        return jnp.moveaxis(t, 1, 0)

    def step(s, inp):
        r_t, w_t, k_t, v_t, a_t, b_t = inp
        sa = jnp.einsum('bhvk,bhk->bhv', s, a_t)
        s = s * w_t[:, :, None, :] + sa[..., None] * b_t[:, :, None, :] + v_t[..., None] * k_t[:, :, None, :]
        return s, jnp.einsum('bhvk,bhk->bhv', s, r_t)

    s0 = jnp.zeros((b_, R_HEADS, R_HEAD_DIM, R_HEAD_DIM), jnp.float32)
    _, o = lax.scan(step, s0, (seq(rh), seq(heads(decay)), seq(kh), seq(vh), seq(-kk), seq(kk * ah)))
    o = jnp.moveaxis(o, 0, 1)
    mu = jnp.mean(o, -1, keepdims=True)
    var = jnp.mean(jnp.square(o - mu), -1, keepdims=True)
    o = ((o - mu) * lax.rsqrt(var + R_GN_EPS)).reshape(b_, l_, R_WIDTH) * gn_w + gn_b
    o = o + (jnp.sum(rh * kh * r_k, -1, keepdims=True) * vh).reshape(b_, l_, R_WIDTH)
    return o * g


def setup_inputs(seed: int = 0) -> dict:
    key = jax.random.key(seed)
    ks = iter(jax.random.split(key, 48))
    f32 = jnp.float32
    D = D_MODEL

    def nrm(shape, scale):
        return scale * jax.random.normal(next(ks), shape, f32)

    def uni(shape, lo, hi):
        return jax.random.uniform(next(ks), shape, f32, lo, hi)

    dt0 = jnp.exp(uni((DEPTH, M_HEADS), float(np.log(M_DT_MIN)), float(np.log(M_DT_MAX))))
    return {
        'x': nrm((BATCH, SEQ, D), 1.0),
        'meta': nrm((N_META, D), 1.0),
        'ln1_w': 1.0 + nrm((DEPTH, D), 0.02),
        'ln2_w': 1.0 + nrm((DEPTH, D), 0.02),
        'lnf_w': 1.0 + nrm((D,), 0.02),
        'w_in': nrm((DEPTH, D, IN_COLS), D ** -0.5),
        'w_in_vres': nrm((DEPTH - 1, D, R_MV_RANK), D ** -0.5),
        'hg_lb_logits': nrm((DEPTH, H_WIDTH), 0.5),
        'hg_norm_w': 1.0 + nrm((DEPTH, H_HEADS, H_HEAD_DIM), 0.02),
        'm_conv_w': nrm((DEPTH, M_CONV, M_CONV_CH), M_CONV ** -0.5),
        'm_conv_b': nrm((DEPTH, M_CONV_CH), 0.01),
        'm_dt_bias': dt0 + jnp.log(-jnp.expm1(-dt0)),
        'm_a_log': jnp.log(uni((DEPTH, M_HEADS), 1.0, 16.0)),
        'm_d': 1.0 + nrm((DEPTH, M_HEADS), 0.02),
        'm_norm_w': 1.0 + nrm((DEPTH, M_WIDTH), 0.02),
        'r_mu': uni((DEPTH, R_COLS), 0.0, 1.0),
        'r_mu_vres': uni((DEPTH - 1, R_MV_RANK), 0.0, 1.0),
        'r_w0': uni((DEPTH, R_WIDTH), -6.5, -1.0),
        'r_w2': nrm((DEPTH, R_DECAY_RANK, R_WIDTH), 0.1 * R_DECAY_RANK ** -0.5),
        'r_a0': nrm((DEPTH, R_WIDTH), 0.1),
        'r_a2': nrm((DEPTH, R_AAA_RANK, R_WIDTH), 0.1 * R_AAA_RANK ** -0.5),
        'r_v0': 1.0 + nrm((DEPTH - 1, R_WIDTH), 0.02),
        'r_v2': nrm((DEPTH - 1, R_MV_RANK, R_WIDTH), 0.1 * R_MV_RANK ** -0.5),
        'r_g2': nrm((DEPTH, R_GATE_RANK, R_WIDTH), R_GATE_RANK ** -0.5),
        'r_k_k': 0.85 + nrm((DEPTH, R_WIDTH), 0.02),
        'r_k_a': 1.0 + nrm((DEPTH, R_WIDTH), 0.02),
        'r_r_k': nrm((DEPTH, R_HEADS, R_HEAD_DIM), 0.1),
        'r_gn_w': 1.0 + nrm((DEPTH, R_WIDTH), 0.02),
        'r_gn_b': nrm((DEPTH, R_WIDTH), 0.01),
        'w_up_h': nrm((DEPTH, H_WIDTH, D), H_WIDTH ** -0.5),
        'w_up_m': nrm((DEPTH, M_WIDTH, D), M_WIDTH ** -0.5),
        'w_up_r': nrm((DEPTH, R_WIDTH, D), R_WIDTH ** -0.5),
        'w_out': nrm((DEPTH, D, D), D ** -0.5),
        'w_mlp_in': nrm((DEPTH, D, MLP_HIDDEN), D ** -0.5),
        'w_mlp_out': nrm((DEPTH, MLP_HIDDEN, D), MLP_HIDDEN ** -0.5),
    }


def reference(x, meta, ln1_w, ln2_w, lnf_w, w_in, w_in_vres, hg_lb_logits, hg_norm_w,
              m_conv_w, m_conv_b, m_dt_bias, m_a_log, m_d, m_norm_w,
              r_mu, r_mu_vres, r_w0, r_w2, r_a0, r_a2, r_v0, r_v2, r_g2, r_k_k, r_k_a, r_r_k, r_gn_w, r_gn_b,
              w_up_h, w_up_m, w_up_r, w_out, w_mlp_in, w_mlp_out):
    f32 = jnp.float32
    b_, _, d_ = x.shape
    n_pad = CHUNK - N_META
    h = jnp.concatenate([jnp.zeros((b_, n_pad, d_), x.dtype),
                         jnp.broadcast_to(meta.astype(x.dtype), (b_, N_META, d_)), x], axis=1)
    l_ = h.shape[1]
    valid = (jnp.arange(l_) >= n_pad).astype(f32)[None, :, None]

    lb_sm = jax.nn.softmax(hg_lb_logits.astype(f32), axis=0)
    lb_all = jnp.cumsum(lb_sm, axis=0) - lb_sm[0]

    v_first = None
    for l in range(DEPTH):
        u = rmsnorm(h, ln1_w[l])
        w_comb = w_in[l] if l == 0 else jnp.concatenate([w_in[l], w_in_vres[l - 1]], axis=1)
        p = u @ w_comb
        p_h, p_m, p_r, p_gate, p_vres = jnp.split(
            p, [H_COLS, H_COLS + M_COLS, H_COLS + M_COLS + R_COLS, IN_COLS], axis=-1)

        q_raw, f_raw, i_raw, g_raw = jnp.split(p_h, 4, axis=-1)
        o_h = hgrn2_mixer(q_raw, f_raw, i_raw, g_raw, lb_all[l], hg_norm_w[l], valid)

        z, xbc, dt_raw = jnp.split(p_m, [M_WIDTH, M_WIDTH + M_CONV_CH], axis=-1)
        o_m = mamba2_mixer(z, xbc, dt_raw, m_conv_w[l], m_conv_b[l], m_dt_bias[l], m_a_log[l],
                           m_d[l], m_norm_w[l], valid)

        pr = token_shift(p_r.astype(f32), r_mu[l])
        r, k, v, wl, al, gl = jnp.split(pr, [R_WIDTH, 2 * R_WIDTH, 3 * R_WIDTH, 3 * R_WIDTH + R_DECAY_RANK,
                                             3 * R_WIDTH + R_DECAY_RANK + R_AAA_RANK], axis=-1)
        w_log = -jax.nn.softplus(-(r_w0[l] + jnp.tanh(wl) @ r_w2[l])) - 0.5
        decay = jnp.exp(-jnp.exp(w_log))
        a = jax.nn.sigmoid(r_a0[l] + al @ r_a2[l])
        if l == 0:
            v_first = v
        else:
            vl = token_shift(p_vres.astype(f32), r_mu_vres[l - 1])
            v = v + (v_first - v) * jax.nn.sigmoid(r_v0[l - 1] + vl @ r_v2[l - 1])
        g = jax.nn.sigmoid(gl) @ r_g2[l]
        o_r = rwkv7_mixer(r, decay, k, v, a, g, r_k_k[l], r_k_a[l], r_r_k[l], r_gn_w[l], r_gn_b[l], valid)

        gates = jax.nn.sigmoid(p_gate.astype(f32)).reshape(b_, l_, N_BRANCH, d_)
        mixed = (gates[:, :, 0] * (o_h @ w_up_h[l]) + gates[:, :, 1] * (o_m @ w_up_m[l])
                 + gates[:, :, 2] * (o_r @ w_up_r[l]))
        h = h + (valid * (mixed @ w_out[l])).astype(h.dtype)

        u = rmsnorm(h, ln2_w[l])
        h = h + (valid * (jnp.square(jax.nn.relu(u @ w_mlp_in[l])) @ w_mlp_out[l])).astype(h.dtype)

    return rmsnorm(h, lnf_w)[:, CHUNK:]
```

```python
import contextlib
import numpy as np
import concourse.bass as bass
import concourse.mybir as mybir
from concourse.bass_utils import run_bass_kernel_spmd

F32 = mybir.dt.float32
BF16 = mybir.dt.bfloat16
AF = mybir.ActivationFunctionType
ALU = mybir.AluOpType
AX = mybir.AxisListType

D = 2048
SEQ = 16384
CH = 64
LTOT = SEQ + CH
NCHUNK = LTOT // CH
EPS = 1e-6
GN_EPS = 64e-5


class View:
    __slots__ = ("buf", "ap")

    def __init__(self, buf, ap):
        self.buf = buf
        self.ap = ap

    def __getitem__(self, idx):
        return View(self.buf, self.ap[idx])

    def re(self, pat, **kw):
        return View(self.buf, self.ap.rearrange(pat, **kw))

    def bc(self, shape):
        return View(self.buf, self.ap.to_broadcast(list(shape)))

    def un(self, d):
        return View(self.buf, self.ap.unsqueeze(d))


class Buf:
    __slots__ = ("ap", "w", "r", "name")

    def __init__(self, ap, name=""):
        self.ap = ap
        self.w = None
        self.r = []
        self.name = name

    def __getitem__(self, idx):
        return View(self, self.ap[idx])


def _a(x):
    return x.ap if isinstance(x, View) else x


def _bufs(*xs):
    return [x.buf for x in xs if isinstance(x, View)]


class Prog:
    ENG = ("sync", "scalar", "vector", "gpsimd", "tensor")
    DMAQ = ("sync", "scalar", "gpsimd")

    def __init__(self, nc, n_dma_sems=6):
        self.nc = nc
        self.es = contextlib.ExitStack()
        self.ops = {e: [] for e in self.ENG}
        self.cnt = {e: 0 for e in self.ENG}
        self.sem = {}
        for e in self.ENG:
            self.sem["E_" + e] = self.es.enter_context(nc.semaphore("s_" + e))
        self.seen = {e: {} for e in self.ENG}
        self.dpool = {}
        for q in self.DMAQ:
            lst = []
            for i in range(n_dma_sems):
                k = f"D_{q}_{i}"
                self.sem[k] = self.es.enter_context(nc.semaphore(k))
                lst.append([k, 0])
            self.dpool[q] = lst
        self.drr = {q: 0 for q in self.DMAQ}
        self.banks = []
        self.bank_i = 0

    def sb(self, name, shape, dt=F32):
        return Buf(self.es.enter_context(self.nc.sbuf_tensor("sb_" + name, list(shape), dt)), name)

    def ps(self, name, shape, dt=F32):
        return Buf(self.es.enter_context(self.nc.psum_tensor("ps_" + name, list(shape), dt)), name)

    def make_banks(self, n=8):
        self.banks = [self.ps(f"bank{i}", [128, 512]) for i in range(n)]

    def bank(self):
        b = self.banks[self.bank_i]
        self.bank_i = (self.bank_i + 1) % len(self.banks)
        return b

    def _waits(self, eng, reads, writes):
        deps = []
        for b in reads:
            if b.w is not None:
                deps.append(b.w)
        for b in writes:
            if b.w is not None:
                deps.append(b.w)
            deps.extend(b.r)
        waits = {}
        seen = self.seen[eng]
        own = "E_" + eng
        for (k, v) in deps:
            if k == own and eng == "tensor":
                continue
            if seen.get(k, 0) >= v:
                continue
            if waits.get(k, 0) < v:
                waits[k] = v
        for k, v in waits.items():
            seen[k] = v
        return waits

    def op(self, eng, fn, reads=(), writes=()):
        waits = self._waits(eng, reads, writes)
        self.cnt[eng] += 1
        tok = ("E_" + eng, self.cnt[eng])
        self.ops[eng].append((waits, fn, tok[0], 1))
        for b in reads:
            b.r.append(tok)
        for b in writes:
            b.w = tok
            b.r = []
        return tok

    def dma(self, q, out, in_):
        reads = _bufs(in_)
        writes = _bufs(out)
        pool = self.dpool[q]
        i = self.drr[q]
        self.drr[q] = (i + 1) % len(pool)
        k, v = pool[i]
        waits = self._waits(q, reads, writes)
        if v > 0 and self.seen[q].get(k, 0) < v:
            waits[k] = v
            self.seen[q][k] = v
        pool[i][1] = v + 16
        tok = (k, v + 16)
        oa, ia = _a(out), _a(in_)
        self.ops[q].append((waits, lambda e: e.dma_start(out=oa, in_=ia), k, 16))
        for b in reads:
            b.r.append(tok)
        for b in writes:
            b.w = tok
            b.r = []
        return tok

    def mm(self, out, lhsT, rhs, start=True, stop=True):
        self.op("tensor", lambda e: e.matmul(out.ap, lhsT=lhsT.ap, rhs=rhs.ap, start=start, stop=stop),
                reads=_bufs(lhsT, rhs), writes=_bufs(out))

    def tr(self, out, in_, ident):
        self.op("tensor", lambda e: e.transpose(out=out.ap, in_=in_.ap, identity=ident.ap),
                reads=_bufs(in_, ident), writes=_bufs(out))

    def act(self, out, in_, func, scale=1.0, bias=None, accum=None):
        kw = {}
        if bias is not None:
            kw["bias"] = _a(bias)
        if accum is not None:
            kw["accum_out"] = _a(accum)
        sc = _a(scale)
        self.op("scalar", lambda e: e.activation(out=out.ap, in_=in_.ap, func=func, scale=sc, **kw),
                reads=_bufs(in_, bias, scale), writes=_bufs(out, accum))

    def tt(self, out, a, b, op, eng="vector"):
        self.op(eng, lambda e: e.tensor_tensor(out=out.ap, in0=a.ap, in1=b.ap, op=op),
                reads=_bufs(a, b), writes=_bufs(out))

    def ts(self, out, a, s1, op0, s2=None, op1=None, eng="vector"):
        kw = {}
        if op1 is not None:
            kw["op1"] = op1
        x1, x2 = _a(s1), _a(s2)
        self.op(eng, lambda e: e.tensor_scalar(out=out.ap, in0=a.ap, scalar1=x1, scalar2=x2, op0=op0, **kw),
                reads=_bufs(a, s1, s2), writes=_bufs(out))

    def stt(self, out, a, sc, b, op0, op1):
        x = _a(sc)
        self.op("vector", lambda e: e.scalar_tensor_tensor(out=out.ap, in0=a.ap, scalar=x, in1=b.ap, op0=op0, op1=op1),
                reads=_bufs(a, sc, b), writes=_bufs(out))

    def cp(self, out, in_, eng="vector"):
        self.op(eng, lambda e: e.tensor_copy(out=out.ap, in_=in_.ap), reads=_bufs(in_), writes=_bufs(out))

    def scan(self, out, d0, d1, init=0.0):
        self.op("vector", lambda e: e.tensor_tensor_scan(out=out.ap, data0=d0.ap, data1=d1.ap, initial=init,
                                                         op0=ALU.mult, op1=ALU.add),
                reads=_bufs(d0, d1), writes=_bufs(out))

    def red(self, out, in_, op=ALU.add, axis=AX.X):
        self.op("vector", lambda e: e.tensor_reduce(out=out.ap, in_=in_.ap, op=op, axis=axis),
                reads=_bufs(in_), writes=_bufs(out))

    def memset(self, out, val, eng="gpsimd"):
        self.op(eng, lambda e: e.memset(out.ap, val), writes=_bufs(out))

    def build(self):
        for q in self.DMAQ:
            for k, v in self.dpool[q]:
                if v > 0:
                    self.ops[q].append(({k: v}, None, None, 0))
        w = {}
        for e in self.ENG:
            if e != "sync" and self.cnt[e] > 0:
                w["E_" + e] = self.cnt[e]
        if w:
            self.ops["sync"].append((w, None, None, 0))
        sem = self.sem
        ops = self.ops

        def replay(name, eng):
            for (waits, fn, k, n) in ops[name]:
                for wk, wv in waits.items():
                    eng.wait_ge(sem[wk], wv)
                if fn is not None:
                    fn(eng).then_inc(sem[k], n)

        with self.nc.Block() as block:
            @block.sync
            def _(e):
                replay("sync", e)

            @block.scalar
            def _(e):
                replay("scalar", e)

            @block.vector
            def _(e):
                replay("vector", e)

            @block.gpsimd
            def _(e):
                replay("gpsimd", e)

            @block.tensor
            def _(e):
                replay("tensor", e)
        self.es.close()


NC2 = 3332
C_TM = 0
C_Q = 1028
C_F = 1284
C_X = 1540
C_B = 1796
C_C = 1924
C_R = 2052
C_K = 2308
C_V = 2564
C_WL = 2820
C_AL = 2916
C_GL = 3012
C_VL = 3268
OUTW = 772


def build_k2(nchunk, layer, parts="hmr"):
    nc = bass.Bass("TRN2", target_bir_lowering=False)
    L = nchunk * CH

    def din(name, shape):
        return nc.dram_tensor(name, list(shape), F32, kind="ExternalInput").ap()

    h_d = din("h", [L, D])
    wc_d = din("wc", [D, NC2])
    ln1T_d = din("ln1T", [128, 16])
    hgfm_d = din("hgfm", [128, 4])
    hgtm_d = din("hgtm", [64, 768])
    mcv_d = din("mcv", [128, 20])
    mtm_d = din("mtm", [64, 264])
    rp_d = din("rp", [64, 40])
    rmuwa_d = din("rmuwa", [96, 2])
    rmugl_d = din("rmugl", [128, 2])
    rmuvl_d = din("rmuvl", [64, 1])
    w2c_d = din("w2c", [96, 256])
    a2c_d = din("a2c", [96, 256])
    v2c_d = din("v2c", [64, 256])
    g2c_d = din("g2c", [128, 512])
    rtm_d = din("rtm", [64, 512])
    cst_d = din("cst", [128, 448])
    vfin_d = din("vfin", [nchunk, 64, 256])
    lflag_d = din("lflag", [128, 1])
    stin_d = din("stin", [128, 800])
    vtok_d = din("vtok", [64, 1])
    vfm_d = din("vfm", [128, 64])
    stout_d = nc.dram_tensor("stout", [128, 800], F32, kind="ExternalOutput").ap()
    out_d = nc.dram_tensor("out", [L, OUTW], F32, kind="ExternalOutput").ap()
    vfout_d = nc.dram_tensor("vfout", [nchunk, 64, 256], F32, kind="ExternalOutput").ap()

    p = Prog(nc)
    p.make_banks(8)
    sb = p.sb

    cst = sb("cst", [128, 448])
    p.dma("sync", cst[:], cst_d[:, :])
    ident = cst[:, 0:128]
    ones = cst[:, 128:256]
    MU = cst[0:64, 256:320]
    MS = cst[0:64, 320:384]
    MUS = cst[0:64, 384:448]
    ln1T = sb("ln1T", [128, 16]); p.dma("scalar", ln1T[:], ln1T_d[:, :])
    lflag = sb("lflag", [128, 1]); p.dma("scalar", lflag[:], lflag_d[:, :])
    hgfm = sb("hgfm", [128, 4]); p.dma("scalar", hgfm[:], hgfm_d[:, :])
    hnwt = sb("hnwt", [64, 256]); p.dma("scalar", hnwt[:], hgtm_d[:, 512:768])
    mcv = sb("mcv", [128, 4, 5]); p.dma("scalar", mcv[:].re("p a b -> p (a b)"), mcv_d[:, :])
    mtm = sb("mtm", [64, 264]); p.dma("scalar", mtm[:], mtm_d[:, :])
    rp = sb("rp", [64, 10, 4]); p.dma("scalar", rp[:].re("p a b -> p (a b)"), rp_d[:, :])
    rmuwa = sb("rmuwa", [96, 2]); p.dma("scalar", rmuwa[:], rmuwa_d[:, :])
    rmugl = sb("rmugl", [128, 2]); p.dma("scalar", rmugl[:], rmugl_d[:, :])
    rmuvl = sb("rmuvl", [64, 1]); p.dma("scalar", rmuvl[:], rmuvl_d[:, :])
    w2c = sb("w2c", [96, 256]); p.dma("gpsimd", w2c[:], w2c_d[:, :])
    a2c = sb("a2c", [96, 256]); p.dma("gpsimd", a2c[:], a2c_d[:, :])
    v2c = sb("v2c", [64, 256]); p.dma("gpsimd", v2c[:], v2c_d[:, :])
    g2c = sb("g2c", [128, 2, 256]); p.dma("gpsimd", g2c[:].re("p a b -> p (a b)"), g2c_d[:, :])
    rtm = sb("rtm", [64, 512]); p.dma("gpsimd", rtm[:], rtm_d[:, :])

    Wb = sb("Wb", [128, 16, NC2], BF16)
    QW = NC2 // 7
    wst = [sb("wst0", [128, QW]), sb("wst1", [128, QW])]
    it = 0
    for kc in range(16):
        for qq in range(7):
            s = wst[it % 2]
            p.dma("sync" if it % 2 == 0 else "gpsimd", s[:], wc_d[kc * 128:(kc + 1) * 128, qq * QW:(qq + 1) * QW])
            if it % 2 == 0:
                p.cp(Wb[:, kc, qq * QW:(qq + 1) * QW], s[:], eng="vector")
            else:
                p.act(Wb[:, kc, qq * QW:(qq + 1) * QW], s[:], AF.Copy)
            it += 1

    hnw = hnwt[:]
    abc = sb("abc", [64, 4])
    p.act(abc[:], mtm[:, 4:8], AF.Exp)
    p.ts(abc[:], abc[:], -1.0, ALU.mult)
    dtb = mtm[:, 0:4]
    dbc = mtm[:, 8:264]

    S_h = sb("S_h", [128, 2, 128]); H_m = sb("H_m", [128, 4, 64]); ST_r = sb("ST_r", [64, 4, 64])
    XC = sb("XC", [128, 4, 67]); R3 = sb("R3", [64, 12, 65])
    WL = sb("WL", [96, 65]); AL = sb("AL", [96, 65]); GL = sb("GL", [128, 2, 65]); VL = sb("VL", [64, 65])
    vtok = sb("vtok", [64, 1]); p.dma("scalar", vtok[:], vtok_d[:, :])
    vfm = sb("vfm", [128, 64]); p.dma("scalar", vfm[:], vfm_d[:, :])

    STT = sb("STT", [128, 800])

    def state_io(load):
        regs = [(S_h[:].re("p a t -> p (a t)"), 128, slice(0, 256)),
                (H_m[:].re("p a t -> p (a t)"), 128, slice(256, 512)),
                (ST_r[:].re("p a t -> p (a t)"), 64, slice(512, 768)),
                (XC[:, :, 0:3], 128, slice(768, 780)),
                (R3[:, :, 0:1], 64, slice(780, 792)),
                (WL[:, 0:1], 96, slice(792, 793)), (AL[:, 0:1], 96, slice(793, 794)),
                (GL[:, :, 0:1], 128, slice(794, 796)), (VL[:, 0:1], 64, slice(796, 797))]
        if load:
            p.dma("sync", STT[:], stin_d[:, :])
        else:
            p.memset(STT[:], 0.0, eng="vector")
        for (v, np_, cs) in regs:
            t_ = STT[0:np_, cs]
            if len(v.ap.shape) == 3:
                t_ = t_.re("p (a w) -> p a w", a=v.ap.shape[1])
            if load:
                p.cp(v, t_)
            else:
                p.cp(t_, v)
        if not load:
            p.dma("sync", stout_d[:, :], STT[:])

    state_io(True)

    NPOOL = 31
    pool = [sb(f"t{i}", [64, 256]) for i in range(NPOOL)]
    pi = [0]

    def T2():
        b_ = pool[pi[0]]; pi[0] += 1
        return b_[:]

    def T4():
        return T2().re("p (a t) -> p a t", a=4)

    lbfm = sb("lbfm", [128, 2]); omlfm = sb("omlfm", [128, 2])
    lbtm = sb("lbtm", [64, 256]); omltm = sb("omltm", [64, 256])
    if layer == 0:
        p.memset(lbfm[:], 0.0); p.memset(omlfm[:], 1.0)
        p.memset(lbtm[:], 0.0); p.memset(omltm[:], 1.0)
    else:
        tmpa = sb("tmpa", [128, 2]); tmpb = pool[0][:]; l0t = pool[1][:]; l1t = pool[2][:]
        p.dma("scalar", l0t, hgtm_d[:, 0:256]); p.dma("scalar", l1t, hgtm_d[:, 256:512])
        p.tt(tmpa[:], hgfm[:, 2:4], hgfm[:, 0:2], ALU.subtract)
        p.act(lbfm[:], tmpa[:], AF.Sigmoid)
        p.ts(lbfm[:], lbfm[:], lflag[:, 0:1], ALU.mult)
        p.ts(omlfm[:], lbfm[:], -1.0, ALU.mult, 1.0, ALU.add)
        p.tt(tmpb, l1t, l0t, ALU.subtract)
        p.act(lbtm[:], tmpb, AF.Sigmoid)
        p.ts(lbtm[:], lbtm[:], lflag[0:64, 0:1], ALU.mult)
        p.ts(omltm[:], lbtm[:], -1.0, ALU.mult, 1.0, ALU.add)
    ht = sb("ht", [64, D]); junk = sb("junk", [64, 256])
    s1 = sb("s1", [64, 8]); s8 = sb("s8", [64, 8])
    uT = sb("uT", [128, 16, 64], BF16)
    OUT = sb("OUT", [64, OUTW])
    sz = sb("sz", [64, 256])
    pi[0] = 0
    fsg = T2(); logf = T2(); ktm = T2(); Vh = T2(); sg = T2(); eg_h = T2(); kend = T2(); og = T2()
    am = T2()[:, 0:128].re("p (a t) -> p a t", a=2)
    qs = sb("qs", [128, 2, 64]); fT = sb("fT", [128, 2, 64]); logfT = sb("logfT", [128, 2, 64]); kT = sb("kT", [128, 2, 64])
    gcT = sb("gcT", [128, 2, 64]); e1 = sb("e1", [128, 2, 64]); e2 = sb("e2", [128, 2, 64]); e3 = sb("e3", [128, 2, 64])
    ngm = sb("ngm", [128, 2])
    qt = sb("qt", [128, 2, 64]); qh = sb("qh", [128, 2, 64]); ktl = sb("ktl", [128, 2, 64])
    pi[0] = 0
    cacc = sb("cacc", [128, 4, 64]); ctmp = sb("ctmp", [128, 4, 64]); XS = sb("XS", [128, 4, 64])
    xsB = sb("xsB", [64, 384]); dt = sb("dt", [64, 4]); dtA = sb("dtA", [64, 4]); Y = T4()
    LT = T4(); EAbc = sb("EAbc", [128, 4, 64]); te = sb("te", [64, 4]); cdbc = sb("cdbc", [128, 4])
    scm = T2()[:, 0:64]; MT = T4(); xc = T4(); xe = T4()
    Cs = sb("Cs", [128, 4, 64]); Htmp = ctmp; y2 = T2()
    pi[0] = 0
    dsh = sb("dsh", [64, 12, 64]); PR = dsh
    dwl = sb("dwl", [96, 64]); tw = sb("tw", [96, 64]); dal = sb("dal", [96, 64]); als = sb("als", [96, 64])
    dgl = sb("dgl", [128, 2, 64]); sgl = sb("sgl", [128, 2, 64]); dvl = sb("dvl", [64, 64]); vls = sb("vls", [64, 64])
    lw = T4(); aa = T4(); sv = T4(); vv = T4()
    vfst = T4()
    kk = T4(); kk2 = T4(); rn = kk2; kkn = T4()
    kh = T4(); beta = T4()
    gc = T4(); eg = T4(); egm = T4(); ein = T4()
    eend = T4()
    AR = sb("AR", [64, 4, 2, 64]); bhat = T4(); khat = T4()
    bkf = sb("bkf", [64, 8, 64]); BK = sb("BK", [64, 8, 64]); Vtm = T4()
    Pn = [T4(), T4()]
    Pt = [T4(), T4()]
    Qt = [T4(), T4()]
    ABRB = sb("ABRB", [64, 4, 2, 64]); AKRK = sb("AKRK", [64, 4, 2, 64])
    Xr = T4(); SA = T4()
    st4 = sb("st4", [64, 4]); st4b = sb("st4b", [64, 4]); oc = T4(); osq = T4()
    prod = T4(); bs = sb("bs", [64, 4]); o2 = osq; gtm = T2()
    MK2 = sb("MK2", [64, 2, 64])
    p.cp(MK2[:, 0, :], MUS); p.cp(MK2[:, 1, :], MU)

    B4 = [64, 4, 64]

    def rstd_from(out, in_, scale, eps):
        p.ts(out, in_, scale, ALU.mult, eps, ALU.add)
        p.act(out, out, AF.Ln)
        p.act(out, out, AF.Exp, scale=-0.5)

    for c in range(nchunk):
        first = (c == 0)
        p.dma("sync", ht[:], h_d[c * CH:(c + 1) * CH, :])
        for q4 in range(8):
            p.act(junk[:], ht[:, q4 * 256:(q4 + 1) * 256], AF.Square, accum=s8[:, q4:q4 + 1])
        p.red(s1[:, 0:1], s8[:])
        rstd_from(s1[:, 1:2], s1[:, 0:1], 1.0 / D, EPS)
        p.ts(ht[:], ht[:], s1[:, 1:2], ALU.mult)
        for half in range(2):
            bk = p.bank()
            for j in range(8):
                kc = half * 8 + j
                p.tr(bk[:, j * 64:(j + 1) * 64], ht[:, kc * 128:(kc + 1) * 128], ident[0:64, 0:64])
            p.tt(uT[:, half * 8:(half + 1) * 8, :], bk[:, :].re("p (a t) -> p a t", a=8),
                 ln1T[:, half * 8:(half + 1) * 8].un(2).bc([128, 8, 64]), ALU.mult)

        def proj_tm(bk, ncols, c0):
            for kc in range(16):
                p.mm(bk[0:64, 0:ncols], uT[:, kc, :], Wb[:, kc, c0:c0 + ncols], start=(kc == 0), stop=(kc == 15))

        def proj_fm(dst, m, c0):
            for kc in range(16):
                p.mm(dst, Wb[:, kc, c0:c0 + m], uT[:, kc, :], start=(kc == 0), stop=(kc == 15))

        if parts != 'hmr':
            p.memset(OUT[:], 0.0, eng='vector')
        if 'h' in parts:
            bk = p.bank(); proj_tm(bk, 512, C_TM)
            p.act(fsg[:], bk[0:64, 0:256], AF.Sigmoid)
            p.act(Vh[:], bk[0:64, 256:512], AF.Copy)
            p.tt(fsg[:], fsg[:], omltm[:], ALU.mult)
            p.tt(fsg[:], fsg[:], lbtm[:], ALU.add)
            p.act(logf[:], fsg[:], AF.Ln)
            p.ts(ktm[:], fsg[:], -1.0, ALU.mult, 1.0, ALU.add)
            if first:
                p.ts(ktm[:], ktm[:], vtok[:, 0:1], ALU.mult)
            bk = p.bank(); proj_tm(bk, 512, C_TM + 512)
            p.act(sg[:], bk[0:64, 0:256], AF.Sigmoid)
            p.act(sz[:], bk[0:64, 256:512], AF.Silu)
            bk = p.bank()
            for hh in range(2):
                proj_fm(bk[:, hh * 64:(hh + 1) * 64], 128, C_Q + hh * 128)
                proj_fm(bk[:, 128 + hh * 64:128 + (hh + 1) * 64], 128, C_F + hh * 128)
            p.act(qs[:].re("p a t -> p (a t)"), bk[:, 0:128], AF.Silu)
            p.act(fT[:].re("p a t -> p (a t)"), bk[:, 128:256], AF.Sigmoid)
            p.tt(fT[:], fT[:], omlfm[:].un(2).bc([128, 2, 64]), ALU.mult)
            p.tt(fT[:], fT[:], lbfm[:].un(2).bc([128, 2, 64]), ALU.add)
            p.act(logfT[:], fT[:], AF.Ln)
            p.ts(kT[:], fT[:], -1.0, ALU.mult, 1.0, ALU.add)
            if first:
                p.tt(kT[:], kT[:], vfm[:].un(1).bc([128, 2, 64]), ALU.mult)
            for hh in range(2):
                p.scan(gcT[:, hh, :], ones[:, 0:64], logfT[:, hh, :])
            p.act(e3[:], gcT[:], AF.Exp)
            p.ts(ngm[:], gcT[:, :, 31], -1.0, ALU.mult)
            for hh in range(2):
                p.act(e1[:, hh, :], gcT[:, hh, :], AF.Exp, bias=ngm[:, hh:hh + 1])
                p.act(e2[:, hh, :], gcT[:, hh, :], AF.Exp, scale=-1.0, bias=gcT[:, hh, 31:32])
            p.tt(qt[:], qs[:], e1[:], ALU.mult)
            p.tt(qh[:], qs[:], e3[:], ALU.mult)
            p.tt(ktl[:], kT[:], e2[:], ALU.mult)
            bk = p.bank()
            p.mm(bk[0:64, 0:256], MS, logf[:])
            p.act(eg_h[:], bk[0:64, 0:256], AF.Exp)
            p.tt(kend[:], ktm[:], eg_h[:], ALU.mult)
            bk = p.bank()
            for hh in range(2):
                p.mm(bk[0:64, hh * 64:(hh + 1) * 64], ktl[:, hh, :], qt[:, hh, :])
            p.tt(am[:], bk[0:64, 0:128].re("p (a t) -> p a t", a=2), MU.un(1).bc([64, 2, 64]), ALU.mult)
            bko = p.bank()
            for hh in range(2):
                p.mm(bko[0:64, hh * 128:(hh + 1) * 128], am[:, hh, :], Vh[:, hh * 128:(hh + 1) * 128], start=True, stop=False)
                p.mm(bko[0:64, hh * 128:(hh + 1) * 128], qh[:, hh, :], S_h[:, hh, :], start=False, stop=True)
            bks = p.bank()
            for hh in range(2):
                p.mm(bks[:, hh * 128:(hh + 1) * 128], kend[:, hh * 128:(hh + 1) * 128], Vh[:, hh * 128:(hh + 1) * 128])
            for hh in range(2):
                p.stt(S_h[:, hh, :], S_h[:, hh, :], e3[:, hh, 63:64], bks[:, hh * 128:(hh + 1) * 128], ALU.mult, ALU.add)
            p.tt(og[:], bko[0:64, 0:256], sg[:], ALU.mult)
            for hh in range(2):
                p.act(junk[:, hh * 128:(hh + 1) * 128], og[:, hh * 128:(hh + 1) * 128], AF.Square, accum=s1[:, 2 + hh:3 + hh])
            rstd_from(s1[:, 4:6], s1[:, 2:4], 1.0 / 128, EPS)
            for hh in range(2):
                p.stt(OUT[:, hh * 128:(hh + 1) * 128], og[:, hh * 128:(hh + 1) * 128], s1[:, 4 + hh:5 + hh],
                      hnw[:, hh * 128:(hh + 1) * 128], ALU.mult, ALU.mult)

        if 'm' in parts:
            bk = p.bank()
            proj_tm(bk, 64, C_TM + 1024 - 60)
            p.tt(dt[:], bk[0:64, 60:64], dtb, ALU.add)
            p.act(dt[:], dt[:], AF.Exp)
            p.ts(dt[:], dt[:], 1.0, ALU.add)
            p.act(dt[:], dt[:], AF.Ln)
            if first:
                p.ts(dt[:], dt[:], vtok[:, 0:1], ALU.mult)
            p.tt(dtA[:], dt[:], abc[:], ALU.mult)
            bk = p.bank()
            for blk in range(2):
                proj_fm(bk[:, blk * 64:(blk + 1) * 64], 128, C_X + blk * 128)
            proj_fm(bk[:, 128:192], 128, C_B)
            proj_fm(bk[:, 192:256], 128, C_C)
            p.act(XC[:, :, 3:67], bk[:, 0:256].re("p (a t) -> p a t", a=4), AF.Copy)
            p.tt(cacc[:], XC[:, :, 0:64], mcv[:, :, 0:1].bc([128, 4, 64]), ALU.mult)
            for m in range(1, 4):
                p.tt(ctmp[:], XC[:, :, m:m + 64], mcv[:, :, m:m + 1].bc([128, 4, 64]), ALU.mult)
                p.tt(cacc[:], cacc[:], ctmp[:], ALU.add)
            p.tt(cacc[:], cacc[:], mcv[:, :, 4:5].bc([128, 4, 64]), ALU.add)
            p.act(XS[:], cacc[:], AF.Silu)
            p.cp(XC[:, :, 0:3], XC[:, :, 64:67])
            bk = p.bank()
            for blk in range(3):
                p.tr(bk[0:64, blk * 128:(blk + 1) * 128], XS[:, blk, :], ident)
            p.act(xsB[:], bk[0:64, 0:384], AF.Copy)
            xs4 = xsB[:, 0:256].re("p (a t) -> p a t", a=4)
            Btm = xsB[:, 256:384]
            for hh in range(4):
                p.ts(Y[:, hh, :], MU, dtA[:, hh:hh + 1], ALU.mult)
            bk = p.bank()
            p.mm(bk[0:64, 0:256], MS, Y[:].re("p a t -> p (a t)"))
            p.act(LT[:].re("p a t -> p (a t)"), bk[0:64, 0:256], AF.Exp)
            p.cp(te[:], LT[:, :, 63])
            bk = p.bank()
            p.mm(bk[:, 0:256], ones[0:64, :], Y[:].re("p a t -> p (a t)"))
            p.act(EAbc[:].re("p a t -> p (a t)"), bk[:, 0:256], AF.Exp)
            p.cp(cdbc[:], EAbc[:, :, 63])
            bk = p.bank()
            p.mm(bk[0:64, 0:64], XS[:, 2, :], XS[:, 3, :])
            p.tt(scm[:], bk[0:64, 0:64], MU, ALU.mult)
            p.tt(MT[:], LT[:], scm[:].un(1).bc(B4), ALU.mult)
            p.tt(xc[:], xs4, dt[:].un(2).bc(B4), ALU.mult)
            p.tt(xe[:], xc[:], te[:].un(2).bc(B4), ALU.mult)
            p.tt(Cs[:], EAbc[:], XS[:, 3:4, :].bc([128, 4, 64]), ALU.mult)
            bky = p.bank()
            for hh in range(4):
                p.mm(bky[0:64, hh * 64:(hh + 1) * 64], MT[:, hh, :], xc[:, hh, :], start=True, stop=False)
                p.mm(bky[0:64, hh * 64:(hh + 1) * 64], Cs[:, hh, :], H_m[:, hh, :], start=False, stop=True)
            bkh = p.bank()
            p.mm(bkh[:, 0:256], Btm, xe[:].re("p a t -> p (a t)"))
            p.tt(Htmp[:], H_m[:], cdbc[:].un(2).bc([128, 4, 64]), ALU.mult)
            p.tt(H_m[:], Htmp[:], bkh[:, 0:256].re("p (a t) -> p a t", a=4), ALU.add)
            p.tt(y2[:], xsB[:, 0:256], dbc, ALU.mult)
            p.tt(y2[:], y2[:], bky[0:64, 0:256], ALU.add)
            p.tt(OUT[:, 256:512], y2[:], sz[:], ALU.mult)
            p.act(junk[:, 0:256], OUT[:, 256:512], AF.Square, accum=OUT[:, 768:769])

        if 'r' in parts:
            bk1 = p.bank()
            for hh in range(4):
                proj_fm(bk1[0:64, hh * 64:(hh + 1) * 64], 64, C_R + hh * 64)
                proj_fm(bk1[0:64, 256 + hh * 64:256 + (hh + 1) * 64], 64, C_K + hh * 64)
            bk2 = p.bank()
            for hh in range(4):
                proj_fm(bk2[0:64, hh * 64:(hh + 1) * 64], 64, C_V + hh * 64)
            proj_fm(bk2[0:96, 256:320], 96, C_WL)
            proj_fm(bk2[0:96, 320:384], 96, C_AL)
            if layer > 0:
                proj_fm(bk2[0:64, 384:448], 64, C_VL)
            bk3 = p.bank()
            for blk in range(2):
                proj_fm(bk3[:, blk * 64:(blk + 1) * 64], 128, C_GL + blk * 128)
            p.act(R3[:, 0:8, 1:65], bk1[0:64, :].re("p (a t) -> p a t", a=8), AF.Copy)
            p.act(R3[:, 8:12, 1:65], bk2[0:64, 0:256].re("p (a t) -> p a t", a=4), AF.Copy)
            p.act(WL[:, 1:65], bk2[0:96, 256:320], AF.Copy)
            p.act(AL[:, 1:65], bk2[0:96, 320:384], AF.Copy)
            if layer > 0:
                p.act(VL[:, 1:65], bk2[0:64, 384:448], AF.Copy)
            p.act(GL[:, :, 1:65], bk3[:, 0:128].re("p (a t) -> p a t", a=2), AF.Copy)
            mu3 = rp[:, 0:3, :].un(3).bc([64, 3, 4, 64])
            p.tt(dsh[:], R3[:, :, 0:64], R3[:, :, 1:65], ALU.subtract)
            p.tt(dsh[:].re("p (a b) t -> p a b t", a=3), dsh[:].re("p (a b) t -> p a b t", a=3), mu3, ALU.mult)
            p.tt(PR[:], dsh[:], R3[:, :, 1:65], ALU.add)
            p.cp(R3[:, :, 0:1], R3[:, :, 64:65])
            p.tt(dwl[:], WL[:, 0:64], WL[:, 1:65], ALU.subtract)
            p.stt(dwl[:], dwl[:], rmuwa[:, 0:1], WL[:, 1:65], ALU.mult, ALU.add)
            p.cp(WL[:, 0:1], WL[:, 64:65])
            p.act(tw[:], dwl[:], AF.Tanh)
            p.tt(dal[:], AL[:, 0:64], AL[:, 1:65], ALU.subtract)
            p.stt(als[:], dal[:], rmuwa[:, 1:2], AL[:, 1:65], ALU.mult, ALU.add)
            p.cp(AL[:, 0:1], AL[:, 64:65])
            p.tt(dgl[:], GL[:, :, 0:64], GL[:, :, 1:65], ALU.subtract)
            p.tt(dgl[:], dgl[:], rmugl[:].un(2).bc([128, 2, 64]), ALU.mult)
            p.tt(dgl[:], dgl[:], GL[:, :, 1:65], ALU.add)
            p.cp(GL[:, :, 0:1], GL[:, :, 64:65])
            p.act(sgl[:], dgl[:], AF.Sigmoid)
            if layer > 0:
                p.tt(dvl[:], VL[:, 0:64], VL[:, 1:65], ALU.subtract)
                p.stt(vls[:], dvl[:], rmuvl[:, 0:1], VL[:, 1:65], ALU.mult, ALU.add)
                p.cp(VL[:, 0:1], VL[:, 64:65])
            r_ = PR[:, 0:4, :]; k_ = PR[:, 4:8, :]; v_ = PR[:, 8:12, :]
            bk = p.bank()
            for hh in range(4):
                p.mm(bk[0:64, hh * 64:(hh + 1) * 64], w2c[:, hh * 64:(hh + 1) * 64], tw[:])
                p.mm(bk[0:64, 256 + hh * 64:256 + (hh + 1) * 64], a2c[:, hh * 64:(hh + 1) * 64], als[:])
            p.tt(lw[:], bk[0:64, 0:256].re("p (a t) -> p a t", a=4), rp[:, 3, :].un(2).bc(B4), ALU.add)
            p.act(lw[:], lw[:], AF.Sigmoid)
            p.ts(lw[:], lw[:], -0.6065306597126334, ALU.mult)
            p.tt(aa[:], bk[0:64, 256:512].re("p (a t) -> p a t", a=4), rp[:, 4, :].un(2).bc(B4), ALU.add)
            p.act(aa[:], aa[:], AF.Sigmoid)
            bkg = p.bank()
            for kc in range(2):
                p.mm(bkg[0:64, 0:256], sgl[:, kc, :], g2c[:, kc, :], start=(kc == 0), stop=(kc == 1))
            p.act(gtm[:], bkg[0:64, 0:256], AF.Copy)
            if layer == 0 or layer == 2:
                p.cp(vv[:], v_)
                p.dma("gpsimd", vfout_d[c], vv[:].re("p a t -> p (a t)"))
            if layer > 0:
                p.dma("gpsimd", vfst[:].re("p a t -> p (a t)"), vfin_d[c])
                for hh in range(4):
                    p.mm(bkg[0:64, 256 + hh * 64:256 + (hh + 1) * 64], v2c[:, hh * 64:(hh + 1) * 64], vls[:])
                p.tt(sv[:], bkg[0:64, 256:512].re("p (a t) -> p a t", a=4), rp[:, 5, :].un(2).bc(B4), ALU.add)
                p.act(sv[:], sv[:], AF.Sigmoid)
                p.ts(sv[:], sv[:], lflag[0:64, 0:1], ALU.mult)
                p.tt(vv[:], vfst[:], v_, ALU.subtract)
                p.tt(vv[:], vv[:], sv[:], ALU.mult)
                p.tt(vv[:], vv[:], v_, ALU.add)
            p.tt(kk[:], k_, rp[:, 6, :].un(2).bc(B4), ALU.mult)
            p.tt(kk2[:], kk[:], kk[:], ALU.mult)
            bk = p.bank()
            p.mm(bk[0:64, 0:256], ones[0:64, 0:64], kk2[:].re("p a t -> p (a t)"))
            p.ts(rn[:].re("p a t -> p (a t)"), bk[0:64, 0:256], 1e-24, ALU.max)
            p.act(rn[:], rn[:], AF.Ln)
            p.act(rn[:], rn[:], AF.Exp, scale=-0.5)
            p.tt(kkn[:], kk[:], rn[:], ALU.mult)
            p.ts(kh[:], aa[:], -1.0, ALU.add)
            p.tt(kh[:], kh[:], rp[:, 7, :].un(2).bc(B4), ALU.mult)
            p.ts(kh[:], kh[:], 1.0, ALU.add)
            p.tt(kh[:], kh[:], k_, ALU.mult)
            if first:
                p.tt(kh[:], kh[:], vfm[0:64, :].un(1).bc(B4), ALU.mult)
            p.tt(beta[:], kkn[:], aa[:], ALU.mult)
            for hh in range(4):
                p.scan(gc[:, hh, :], ones[0:64, 0:64], lw[:, hh, :])
            p.act(eg[:], gc[:], AF.Exp)
            p.tt(egm[:], gc[:], lw[:], ALU.subtract)
            p.act(egm[:], egm[:], AF.Exp)
            p.act(ein[:], gc[:], AF.Exp, scale=-1.0)
            for hh in range(4):
                p.act(eend[:, hh, :], gc[:, hh, :], AF.Exp, scale=-1.0, bias=gc[:, hh, 63:64])
            p.stt(AR[:, :, 0, :], kkn[:], -1.0, egm[:], ALU.mult, ALU.mult)
            p.tt(AR[:, :, 1, :], r_, eg[:], ALU.mult)
            p.tt(bhat[:], beta[:], ein[:], ALU.mult)
            p.tt(khat[:], kh[:], ein[:], ALU.mult)
            p.tt(bkf[:, 0:4, :], beta[:], eend[:], ALU.mult)
            p.tt(bkf[:, 4:8, :], kh[:], eend[:], ALU.mult)
            bk = p.bank()
            for j in range(8):
                p.tr(bk[0:64, j * 64:(j + 1) * 64], bkf[:, j, :], ident[0:64, 0:64])
            p.act(BK[:].re("p a t -> p (a t)"), bk[0:64, :], AF.Copy)
            bk = p.bank()
            for hh in range(4):
                p.tr(bk[0:64, hh * 64:(hh + 1) * 64], vv[:, hh, :], ident[0:64, 0:64])
            p.act(Vtm[:].re("p a t -> p (a t)"), bk[0:64, 0:256], AF.Copy)
            p.tt(prod[:], r_, kh[:], ALU.mult)
            p.tt(prod[:], prod[:], rp[:, 8, :].un(2).bc(B4), ALU.mult)
            for hh in range(4):
                p.mm(bk[0:64, 256 + 64 * hh:320 + 64 * hh], prod[:, hh, :], ones[0:64, 0:64])
            p.act(bs[:], bk[0:64, 256:512:64], AF.Copy)
            bka = p.bank(); bkb = p.bank(); bkc = p.bank()
            for hh in range(4):
                p.mm(bka[0:64, hh * 64:(hh + 1) * 64], AR[:, hh, 0, :], bhat[:, hh, :])
                p.mm(bkb[0:64, hh * 128:(hh + 1) * 128], bhat[:, hh, :], AR[:, hh, :, :].re("p a t -> p (a t)"))
                p.mm(bkc[0:64, hh * 128:(hh + 1) * 128], khat[:, hh, :], AR[:, hh, :, :].re("p a t -> p (a t)"))
            p.tt(Pn[0][:], bka[0:64, 0:256].re("p (a t) -> p a t", a=4), MS.un(1).bc(B4), ALU.mult)
            mk4 = MK2[:].un(1).bc([64, 4, 2, 64])
            p.tt(ABRB[:], bkb[0:64, :].re("p (a b t) -> p a b t", a=4, b=2), mk4, ALU.mult)
            p.tt(AKRK[:], bkc[0:64, :].re("p (a b t) -> p a b t", a=4, b=2), mk4, ALU.mult)
            p.cp(Pt[0][:], ABRB[:, :, 0, :])
            p.tt(Qt[0][:], Pt[0][:], ident[0:64, 0:64].un(1).bc(B4), ALU.add)
            cur = 0
            for lvl in range(5):
                nxt = 1 - cur
                bk = p.bank()
                for hh in range(4):
                    p.mm(bk[0:64, hh * 64:(hh + 1) * 64], Pt[cur][:, hh, :], Pn[cur][:, hh, :])
                p.act(Pn[nxt][:].re("p a t -> p (a t)"), bk[0:64, 0:256], AF.Copy)
                if lvl < 4:
                    for hh in range(4):
                        p.mm(bk[0:64, 256 + hh * 64:256 + (hh + 1) * 64], Pn[cur][:, hh, :], Pt[cur][:, hh, :])
                    p.act(Pt[nxt][:].re("p a t -> p (a t)"), bk[0:64, 256:512], AF.Copy)
                bk = p.bank()
                for hh in range(4):
                    p.mm(bk[0:64, hh * 64:(hh + 1) * 64], Pn[nxt][:, hh, :], Qt[cur][:, hh, :])
                p.tt(Qt[nxt][:], Qt[cur][:], bk[0:64, 0:256].re("p (a t) -> p a t", a=4), ALU.add)
                cur = nxt
            QT = Qt[cur]
            bk = p.bank()
            for hh in range(4):
                p.mm(bk[0:64, hh * 64:(hh + 1) * 64], AR[:, hh, 0, :], ST_r[:, hh, :], start=True, stop=False)
                p.mm(bk[0:64, hh * 64:(hh + 1) * 64], AKRK[:, hh, 0, :], Vtm[:, hh, :], start=False, stop=True)
            p.act(Xr[:].re("p a t -> p (a t)"), bk[0:64, 0:256], AF.Copy)
            for hh in range(4):
                p.mm(bk[0:64, 256 + hh * 64:256 + (hh + 1) * 64], QT[:, hh, :], Xr[:, hh, :])
            p.act(SA[:].re("p a t -> p (a t)"), bk[0:64, 256:512], AF.Copy)
            bko = p.bank()
            for hh in range(4):
                o_ = bko[0:64, hh * 64:(hh + 1) * 64]
                p.mm(o_, AR[:, hh, 1, :], ST_r[:, hh, :], start=True, stop=False)
                p.mm(o_, ABRB[:, hh, 1, :], SA[:, hh, :], start=False, stop=False)
                p.mm(o_, AKRK[:, hh, 1, :], Vtm[:, hh, :], start=False, stop=True)
            for hh in range(4):
                s_ = bko[0:64, 256 + hh * 64:256 + (hh + 1) * 64]
                p.mm(s_, BK[:, hh, :], SA[:, hh, :], start=True, stop=False)
                p.mm(s_, BK[:, 4 + hh, :], Vtm[:, hh, :], start=False, stop=True)
            for hh in range(4):
                p.stt(ST_r[:, hh, :], ST_r[:, hh, :], eg[:, hh, 63:64], bko[0:64, 256 + hh * 64:256 + (hh + 1) * 64],
                      ALU.mult, ALU.add)
            o4 = bko[0:64, 0:256].re("p (a t) -> p a t", a=4)
            p.red(st4[:], o4)
            p.ts(st4[:], st4[:], 1.0 / 64, ALU.mult)
            p.tt(oc[:], o4, st4[:].un(2).bc(B4), ALU.subtract)
            p.tt(osq[:], oc[:], oc[:], ALU.mult)
            p.red(st4b[:], osq[:])
            rstd_from(st4b[:], st4b[:], 1.0 / 64, GN_EPS)
            p.tt(oc[:], oc[:], st4b[:].un(2).bc(B4), ALU.mult)
            p.tt(oc[:], oc[:], rtm[:, 0:256].re("p (a t) -> p a t", a=4), ALU.mult)
            p.tt(oc[:], oc[:], rtm[:, 256:512].re("p (a t) -> p a t", a=4), ALU.add)
            p.tt(o2[:], Vtm[:], bs[:].un(2).bc(B4), ALU.mult)
            p.tt(o2[:], o2[:], oc[:], ALU.add)
            p.tt(OUT[:, 512:768], o2[:].re("p a t -> p (a t)"), gtm[:], ALU.mult)
        p.memset(OUT[:, 769:772], 0.0, eng="vector")
        p.dma("sync", out_d[c * CH:(c + 1) * CH, :], OUT[:])
    state_io(False)
    with nc.allow_non_contiguous_dma(reason="tiny per-partition state columns"):
        p.build()
    return nc


def _consts():
    c = np.zeros((128, 448), np.float32)
    c[:, 0:128] = np.eye(128, dtype=np.float32)
    c[:, 128:256] = 1.0
    pp = np.arange(64)[:, None]
    ff = np.arange(64)[None, :]
    c[0:64, 256:320] = (pp <= ff)
    c[0:64, 320:384] = (pp > ff)
    c[0:64, 384:448] = (pp < ff)
    return c


def k2_params(P, layer, g):
    l = layer
    f32 = np.float32
    w_in = P["w_in"][l]
    hs = slice(256 * g, 256 * g + 256)
    G = g // 2
    cols = []
    cols.append(w_in[:, 1024 + 256 * g:1024 + 256 * g + 256])
    cols.append(w_in[:, 2048 + 256 * g:2048 + 256 * g + 256])
    cols.append(w_in[:, 3072 + 256 * g:3072 + 256 * g + 256])
    cols.append(w_in[:, 4096 + 256 * g:4096 + 256 * g + 256])
    cols.append(w_in[:, 6656 + 4 * g:6656 + 4 * g + 4])
    cols.append(w_in[:, 0 + 256 * g:256 * g + 256])
    cols.append(w_in[:, 1024 + 256 * g:1024 + 256 * g + 256])
    cols.append(w_in[:, 5120 + 256 * g:5120 + 256 * g + 256])
    cols.append(w_in[:, 6144 + 128 * G:6144 + 128 * G + 128])
    cols.append(w_in[:, 6400 + 128 * G:6400 + 128 * G + 128])
    cols.append(w_in[:, 6672 + 256 * g:6672 + 256 * g + 256])
    cols.append(w_in[:, 7696 + 256 * g:7696 + 256 * g + 256])
    cols.append(w_in[:, 8720 + 256 * g:8720 + 256 * g + 256])
    cols.append(w_in[:, 9744:9840])
    cols.append(w_in[:, 9840:9936])
    cols.append(w_in[:, 9936:10192])
    if l > 0:
        cols.append(P["w_in_vres"][l - 1])
    else:
        cols.append(np.zeros((D, 64), f32))
    wc = np.ascontiguousarray(np.concatenate(cols, axis=1), dtype=f32)
    assert wc.shape[1] == NC2
    d = {"wc": wc}
    d["ln1T"] = np.ascontiguousarray(P["ln1_w"][l].reshape(16, 128).T)
    lg = P["hg_lb_logits"]
    hgfm = np.stack([lg[0, 256 * g:256 * g + 128], lg[0, 256 * g + 128:256 * g + 256],
                     lg[1, 256 * g:256 * g + 128], lg[1, 256 * g + 128:256 * g + 256]], axis=1)
    d["hgfm"] = np.ascontiguousarray(hgfm, dtype=f32)
    nw = P["hg_norm_w"][l].reshape(-1)[hs]
    d["hgtm"] = np.ascontiguousarray(np.broadcast_to(np.concatenate([lg[0, hs], lg[1, hs], nw])[None, :], (64, 768)), dtype=f32)
    cw = P["m_conv_w"][l]
    cb = P["m_conv_b"][l]
    chs = [slice(256 * g, 256 * g + 128), slice(256 * g + 128, 256 * g + 256),
           slice(1024 + 128 * G, 1024 + 128 * G + 128), slice(1280 + 128 * G, 1280 + 128 * G + 128)]
    mcv = np.zeros((128, 4, 5), f32)
    for bi, sl in enumerate(chs):
        mcv[:, bi, 0:4] = cw[:, sl].T
        mcv[:, bi, 4] = cb[sl]
    d["mcv"] = mcv.reshape(128, 20)
    mt = np.concatenate([P["m_dt_bias"][l][4 * g:4 * g + 4], P["m_a_log"][l][4 * g:4 * g + 4],
                         np.repeat(P["m_d"][l][4 * g:4 * g + 4], 64)])
    d["mtm"] = np.ascontiguousarray(np.broadcast_to(mt[None, :], (64, 264)), dtype=f32)
    mu = P["r_mu"][l]

    def hk(vec):
        return vec.reshape(4, 64).T

    rp = np.zeros((64, 10, 4), f32)
    rp[:, 0] = hk(mu[0:1024][hs]); rp[:, 1] = hk(mu[1024:2048][hs]); rp[:, 2] = hk(mu[2048:3072][hs])
    rp[:, 3] = hk(P["r_w0"][l][hs]); rp[:, 4] = hk(P["r_a0"][l][hs])
    if l > 0:
        rp[:, 5] = hk(P["r_v0"][l - 1][hs])
    rp[:, 6] = hk(P["r_k_k"][l][hs]); rp[:, 7] = hk(P["r_k_a"][l][hs]); rp[:, 8] = hk(P["r_r_k"][l].reshape(-1)[hs])
    d["rp"] = rp.reshape(64, 40)
    d["rmuwa"] = np.ascontiguousarray(np.stack([mu[3072:3168], mu[3168:3264]], axis=1), dtype=f32)
    d["rmugl"] = np.ascontiguousarray(mu[3264:3520].reshape(2, 128).T, dtype=f32)
    if l > 0:
        d["rmuvl"] = np.ascontiguousarray(P["r_mu_vres"][l - 1].reshape(64, 1), dtype=f32)
        d["v2c"] = np.ascontiguousarray(P["r_v2"][l - 1][:, hs], dtype=f32)
    else:
        d["rmuvl"] = np.zeros((64, 1), f32)
        d["v2c"] = np.zeros((64, 256), f32)
    d["w2c"] = np.ascontiguousarray(P["r_w2"][l][:, hs], dtype=f32)
    d["a2c"] = np.ascontiguousarray(P["r_a2"][l][:, hs], dtype=f32)
    g2 = P["r_g2"][l][:, hs]
    d["g2c"] = np.ascontiguousarray(g2.reshape(2, 128, 256).transpose(1, 0, 2).reshape(128, 512), dtype=f32)
    d["rtm"] = np.ascontiguousarray(np.broadcast_to(np.concatenate([P["r_gn_w"][l][hs], P["r_gn_b"][l][hs]])[None, :], (64, 512)), dtype=f32)
    d["cst"] = _consts()
    return d


NT = 258
NTILE3 = 16
T3 = NT * NTILE3


def build_k3(ntile, last):
    nc = bass.Bass("TRN2", target_bir_lowering=False)
    T = ntile * NT

    def din(name, shape):
        return nc.dram_tensor(name, list(shape), F32, kind="ExternalInput").ap()

    hT_d = din("hT", [D, T])
    vm_d = din("vmask", [128, T])
    oh_d = din("ohT", [1024, T])
    ym_d = din("ymT", [1024, T])
    or_d = din("orT", [1024, T])
    ss_d = din("ssT", [4, T])
    wg_d = din("wg", [24, 128, 16 * 256])
    wup_d = [din("wuph", [8, 128, 8 * 256]), din("wupm", [8, 128, 8 * 256]), din("wupr", [8, 128, 8 * 256])]
    wo_d = din("wo", [8, 128, 16 * 256])
    w1_d = din("w1", [32, 128, 16 * 256])
    w2_d = din("w2", [4 * 8, 128, 16 * 256])
    lnp_d = din("lnp", [128, 56])
    cst_d = din("cst", [128, 448])
    out_d = nc.dram_tensor("outT", [D, T], F32, kind="ExternalOutput").ap()

    p = Prog(nc)
    p.make_banks(8)
    sb = p.sb
    cst = sb("cst", [128, 448]); p.dma("sync", cst[:], cst_d[:, :])
    ones = cst[:, 128:256]
    lnp = sb("lnp", [128, 56]); p.dma("scalar", lnp[:], lnp_d[:, :])
    H = sb("H", [128, 16, NT])
    U = sb("U", [128, 16, NT], BF16)
    MIX = sb("MIX", [128, 16, NT], BF16)
    HID = sb("HID", [128, 32, NT], BF16)
    XB = [sb(f"XB{i}", [128, 8, NT], BF16) for i in range(3)]
    stg = sb("stg", [128, 8, NT])
    vm = sb("vm", [128, NT])
    ss2 = [sb("ss2a", [2, NT]), sb("ss2b", [2, NT])]
    rst = sb("rst", [128, NT]); rst2 = sb("rst2", [128, 2, NT])
    sqt = sb("sqt", [128, NT])
    gat = [sb("gat0", [128, NT]), sb("gat1", [128, NT])]
    acc = sb("acc", [128, NT]); tmp = [sb("tmp0", [128, NT]), sb("tmp1", [128, NT])]
    WS = [sb("ws0", [128, 16, 256]), sb("ws1", [128, 16, 256])]
    WB = [sb("wb0", [128, 16, 256], BF16), sb("wb1", [128, 16, 256], BF16)]
    ctr = {"u": 0, "q": 0, "g": 0}

    def load_unit(w_ap, kc):
        i = ctr["u"] % 2
        ctr["u"] += 1
        q = ("sync", "gpsimd")[ctr["u"] % 2]
        p.dma(q, WS[i][:, 0:kc, :].re("p k c -> p (k c)"), w_ap)
        e = ctr["u"] % 3
        if e == 0:
            p.cp(WB[i][:, 0:kc, :], WS[i][:, 0:kc, :], eng="vector")
        elif e == 1:
            p.cp(WB[i][:, 0:kc, :], WS[i][:, 0:kc, :], eng="gpsimd")
        else:
            p.act(WB[i][:, 0:kc, :], WS[i][:, 0:kc, :], AF.Copy)
        return WB[i]

    def rms_to_U(lncol):
        bk = p.bank()
        for kc in range(16):
            p.act(sqt[:], H[:, kc, :], AF.Square)
            p.mm(bk[:, 0:NT], ones, sqt[:], start=(kc == 0), stop=(kc == 15))
        p.ts(rst[:], bk[:, 0:NT], 1.0 / D, ALU.mult, EPS, ALU.add)
        p.act(rst[:], rst[:], AF.Ln)
        p.act(rst[:], rst[:], AF.Exp, scale=-0.5)
        return rst

    def norm_apply(dst_fn, lncol):
        r = rms_to_U(lncol)
        for kc in range(16):
            p.stt(dst_fn(kc), H[:, kc, :], lnp[:, lncol + kc:lncol + kc + 1], r[:], ALU.mult, ALU.mult)

    for t in range(ntile):
        ts_ = slice(t * NT, (t + 1) * NT)
        p.dma("sync", H[:], hT_d[:, ts_].rearrange("(k p) t -> p k t", p=128))
        p.dma("scalar", vm[:], vm_d[:, ts_])
        p.dma("scalar", ss2[0][:], ss_d[0:2, ts_])
        p.dma("scalar", ss2[1][:], ss_d[2:4, ts_])
        norm_apply(lambda kc: U[:, kc, :], 0)
        p.dma("gpsimd", stg[:], oh_d[:, ts_].rearrange("(k p) t -> p k t", p=128))
        p.cp(XB[0][:], stg[:])
        p.dma("gpsimd", stg[:], or_d[:, ts_].rearrange("(k p) t -> p k t", p=128))
        p.cp(XB[2][:], stg[:])
        p.dma("gpsimd", stg[:], ym_d[:, ts_].rearrange("(k p) t -> p k t", p=128))
        for gi in range(2):
            bk = p.bank()
            p.mm(bk[:, 0:NT], ones[0:2, :], ss2[gi][:])
            p.ts(rst2[:, gi, :], bk[:, 0:NT], 1.0 / 512, ALU.mult, EPS, ALU.add)
        p.act(rst2[:], rst2[:], AF.Ln)
        p.act(rst2[:], rst2[:], AF.Exp, scale=-0.5)
        for kc in range(8):
            p.stt(XB[1][:, kc, :], stg[:, kc, :], lnp[:, 48 + kc:49 + kc], rst2[:, kc // 4, :], ALU.mult, ALU.mult)
        for dp in range(8):
            wgu = []
            for b in range(3):
                wg_u = load_unit(wg_d[b * 8 + dp], 16)
                bkg = [p.bank(), p.bank()]
                for dd in range(2):
                    for kc in range(16):
                        p.mm(bkg[dd][:, 0:NT], wg_u[:, kc, dd * 128:(dd + 1) * 128], U[:, kc, :], start=(kc == 0), stop=(kc == 15))
                    p.act(gat[dd][:], bkg[dd][:, 0:NT], AF.Sigmoid)
                wu_u = load_unit(wup_d[b][dp], 8)
                bku = [p.bank(), p.bank()]
                for dd in range(2):
                    for kc in range(8):
                        p.mm(bku[dd][:, 0:NT], wu_u[:, kc, dd * 128:(dd + 1) * 128], XB[b][:, kc, :], start=(kc == 0), stop=(kc == 7))
                    d = dp * 2 + dd
                    if b == 0:
                        p.tt(tmp[dd][:], gat[dd][:], bku[dd][:, 0:NT], ALU.mult)
                    elif b == 1:
                        p.tt(gat[dd][:], gat[dd][:], bku[dd][:, 0:NT], ALU.mult)
                        p.tt(tmp[dd][:], tmp[dd][:], gat[dd][:], ALU.add)
                    else:
                        p.tt(gat[dd][:], gat[dd][:], bku[dd][:, 0:NT], ALU.mult)
                        p.tt(MIX[:, d, :], tmp[dd][:], gat[dd][:], ALU.add)
        for dp in range(8):
            wu = load_unit(wo_d[dp], 16)
            for dd in range(2):
                d = dp * 2 + dd
                bk = p.bank()
                for kc in range(16):
                    p.mm(bk[:, 0:NT], wu[:, kc, dd * 128:(dd + 1) * 128], MIX[:, kc, :], start=(kc == 0), stop=(kc == 15))
                p.tt(acc[:], bk[:, 0:NT], vm[:], ALU.mult)
                p.tt(H[:, d, :], H[:, d, :], acc[:], ALU.add)
        norm_apply(lambda kc: U[:, kc, :], 16)
        for half in range(2):
            for hp in range(16):
                c0 = half * 4096 + hp * 256
                wu = load_unit(w1_d[half * 16 + hp], 16)
                for dd in range(2):
                    bk = p.bank()
                    for kc in range(16):
                        p.mm(bk[:, 0:NT], wu[:, kc, dd * 128:(dd + 1) * 128], U[:, kc, :], start=(kc == 0), stop=(kc == 15))
                    p.act(gat[dd][:], bk[:, 0:NT], AF.Relu)
                    p.tt(HID[:, hp * 2 + dd, :], gat[dd][:], gat[dd][:], ALU.mult)
            for dp in range(8):
                bks = [p.bank(), p.bank()]
                for part in range(2):
                    r0 = half * 4096 + part * 2048
                    wu = load_unit(w2_d[(half * 2 + part) * 8 + dp], 16)
                    for dd in range(2):
                        for kc in range(16):
                            p.mm(bks[dd][:, 0:NT], wu[:, kc, dd * 128:(dd + 1) * 128], HID[:, part * 16 + kc, :],
                                 start=(part == 0 and kc == 0), stop=(part == 1 and kc == 15))
                for dd in range(2):
                    d = dp * 2 + dd
                    p.tt(acc[:], bks[dd][:, 0:NT], vm[:], ALU.mult)
                    p.tt(H[:, d, :], H[:, d, :], acc[:], ALU.add)
        if last:
            r = rms_to_U(32)
            for kc in range(16):
                p.stt(H[:, kc, :], H[:, kc, :], lnp[:, 32 + kc:33 + kc], r[:], ALU.mult, ALU.mult)
        p.dma("sync", out_d[:, ts_].rearrange("(k p) t -> p k t", p=128), H[:])
    p.build()
    return nc


def k3_params(P, layer, last):
    l = layer
    f32 = np.float32
    d = {}
    def tile_w(W, kr):
        K_, C_ = W.shape
        nrg, ncg, kc = K_ // kr, C_ // 256, kr // 128
        t = W.reshape(nrg, kc, 128, ncg, 256).transpose(0, 3, 2, 1, 4)
        return np.ascontiguousarray(t.reshape(nrg * ncg, 128, kc * 256), dtype=f32)

    d["wg"] = tile_w(P["w_in"][l][:, 10192:16336], 2048)
    d["wuph"] = tile_w(P["w_up_h"][l], 1024)
    d["wupm"] = tile_w(P["w_up_m"][l], 1024)
    d["wupr"] = tile_w(P["w_up_r"][l], 1024)
    d["wo"] = tile_w(P["w_out"][l], 2048)
    d["w1"] = tile_w(P["w_mlp_in"][l], 2048)
    d["w2"] = tile_w(P["w_mlp_out"][l], 2048)
    lnp = np.zeros((128, 56), f32)
    lnp[:, 0:16] = P["ln1_w"][l].reshape(16, 128).T
    lnp[:, 16:32] = P["ln2_w"][l].reshape(16, 128).T
    lnp[:, 32:48] = P["lnf_w"].reshape(16, 128).T
    lnp[:, 48:56] = P["m_norm_w"][l].reshape(8, 128).T
    d["lnp"] = lnp
    d["cst"] = _consts()
    return d


_PROGS = {}
K2_CPS = 86
K2_SPLIT = 3


def _prog(key, fn):
    if key not in _PROGS:
        _PROGS[key] = fn()
    return _PROGS[key]


def run_k2_layer(P, l, h, vf_in, cps, nsplit):
    f32 = np.float32
    nc2 = _prog(("k2", cps), lambda: build_k2(cps, 2))
    rows = cps * CH
    state = [np.zeros((128, 800), f32) for _ in range(8)]
    outs = [[] for _ in range(8)]
    vfo = [[] for _ in range(8)]
    prm = [k2_params(P, l, g) for g in range(4)]
    for j in range(nsplit):
        vt = np.ones((64,), f32)
        if j == 0:
            vt[:48] = 0.0
        in_maps = []
        for core in range(8):
            b, g = core // 4, core % 4
            d = dict(prm[g])
            d["h"] = np.ascontiguousarray(h[b, j * rows:(j + 1) * rows])
            d["vfin"] = np.ascontiguousarray(vf_in[core][j * cps:(j + 1) * cps])
            d["lflag"] = np.full((128, 1), float(l > 0), f32)
            d["stin"] = state[core]
            d["vtok"] = vt.reshape(64, 1).copy()
            d["vfm"] = np.ascontiguousarray(np.broadcast_to(vt[None, :], (128, 64)))
            in_maps.append(d)
        res = run_bass_kernel_spmd(nc2, in_maps, core_ids=list(range(8)))
        for core in range(8):
            outs[core].append(np.asarray(res.results[core]["out"]))
            vfo[core].append(np.asarray(res.results[core]["vfout"]))
            state[core] = np.ascontiguousarray(res.results[core]["stout"])
        del res
    return [np.concatenate(o, axis=0) for o in outs], [np.concatenate(v, axis=0) for v in vfo]


def kernel(**inputs):
    P = {k: np.asarray(v) for k, v in inputs.items()}
    f32 = np.float32
    x = P["x"].astype(f32, copy=False)
    B = x.shape[0]
    L = LTOT
    Lp = K2_CPS * K2_SPLIT * CH
    h = np.zeros((B, Lp, D), f32)
    h[:, 48:64, :] = P["meta"][None]
    h[:, 64:L, :] = x
    NTOK = B * L
    TP = 8 * T3
    vflat = np.zeros((TP,), f32)
    for b in range(B):
        vflat[b * L + 48:(b + 1) * L] = 1.0
    vfout = [np.zeros((K2_CPS * K2_SPLIT, 64, 256), f32) for _ in range(8)]
    for l in range(2):
        outs, vfo = run_k2_layer(P, l, h, vfout, K2_CPS, K2_SPLIT)
        if l == 0:
            vfout = vfo
        oh = np.zeros((TP, 1024), f32); ym = np.zeros((TP, 1024), f32); orr = np.zeros((TP, 1024), f32)
        ssf = np.zeros((4, TP), f32)
        for core in range(8):
            b, g = core // 4, core % 4
            o = outs[core]
            oh[b * L:(b + 1) * L, 256 * g:256 * g + 256] = o[:L, 0:256]
            ym[b * L:(b + 1) * L, 256 * g:256 * g + 256] = o[:L, 256:512]
            orr[b * L:(b + 1) * L, 256 * g:256 * g + 256] = o[:L, 512:768]
            ssf[g, b * L:(b + 1) * L] = o[:L, 768]
        del outs
        hflat = np.zeros((TP, D), f32)
        hflat[:NTOK] = h[:, :L].reshape(NTOK, D)
        last = (l == 1)
        nc3 = _prog(("k3", last), lambda: build_k3(NTILE3, last))
        d3 = k3_params(P, l, last)
        in_maps = []
        for core in range(8):
            r = slice(core * T3, (core + 1) * T3)
            d = dict(d3)
            d["hT"] = np.ascontiguousarray(hflat[r].T)
            d["ohT"] = np.ascontiguousarray(oh[r].T)
            d["ymT"] = np.ascontiguousarray(ym[r].T)
            d["orT"] = np.ascontiguousarray(orr[r].T)
            d["ssT"] = np.ascontiguousarray(ssf[:, r])
            d["vmask"] = np.ascontiguousarray(np.broadcast_to(vflat[r][None, :], (128, T3)))
            in_maps.append(d)
        del oh, ym, orr
        res3 = run_bass_kernel_spmd(nc3, in_maps, core_ids=list(range(8)))
        for core in range(8):
            hflat[core * T3:(core + 1) * T3] = res3.results[core]["outT"].T
        del res3
        h = np.zeros((B, Lp, D), f32)
        h[:, :L] = hflat[:NTOK].reshape(B, L, D)
    return np.ascontiguousarray(h[:, CH:L, :])
```
